# Optimizing a Trainium2 kernel written in Bass

```python
import math
import jax, jax.numpy as jnp
from jax import lax
import numpy as np

D_MODEL = 2048
BATCH = 2
SEQ = 8192
DEPTH = 4

GRID_W = 64
CTX_LEN = 256
CONV_K = 3
SC_WIDTH = D_MODEL
MLA_HEADS = 16
Q_LORA = 512
KV_LORA = 512
QK_NOPE = 128
QK_ROPE = 64
V_DIM = 128
ROPE_BASE = 10000.0
Q_BLOCK = 128
MLA_SCALE = (QK_NOPE + QK_ROPE) ** -0.5
SSM_INNER = 2 * D_MODEL
SSM_HEADDIM = 64
SSM_HEADS = SSM_INNER // SSM_HEADDIM
SSM_GROUPS = 8
SSM_STATE = 128
SSM_CHUNK = 128
SSM_CONV_CH = SSM_INNER + 2 * SSM_GROUPS * SSM_STATE
D_FF = 5632
N_BRANCH = 3
ALPHA = (2 * DEPTH) ** 0.25
BETA = (8 * DEPTH) ** -0.25
EPS = 1e-6
IN_SPLITS = (SC_WIDTH, SC_WIDTH, SC_WIDTH,
             Q_LORA, KV_LORA, QK_ROPE,
             SSM_INNER, SSM_INNER, SSM_GROUPS * SSM_STATE, SSM_GROUPS * SSM_STATE, 2 * SSM_HEADS,
             N_BRANCH * D_MODEL)
D_IN = sum(IN_SPLITS)
SPLIT_POINTS = tuple(int(v) for v in np.cumsum(IN_SPLITS)[:-1])

kernel_name = 'hybrid_flow_backbone'


def layer_norm(x):
    xf = x.astype(jnp.float32)
    mu = xf.mean(-1, keepdims=True)
    var = jnp.square(xf - mu).mean(-1, keepdims=True)
    return ((xf - mu) * lax.rsqrt(var + EPS)).astype(x.dtype)


def post_norm(x, g, b):
    return layer_norm(x) * g + b


def rms_norm(x, g):
    xf = x.astype(jnp.float32)
    y = xf * lax.rsqrt(jnp.square(xf).mean(-1, keepdims=True) + EPS)
    return y.astype(x.dtype) * g


def modulate(x, shift, scale):
    return layer_norm(x) * (1 + scale) + shift


def dwconv(x, w, b=None):
    k = w.shape[0]
    n = x.shape[1]
    pad = k // 2
    xp = jnp.pad(x, ((0, 0), (pad, pad), (0, 0)))
    y = xp[:, 0:n] * w[0]
    for i in range(1, k):
        y = y + xp[:, i:i + n] * w[i]
    return y if b is None else y + b


def axial_rope_tables(n_tokens):
    rows_n = n_tokens // GRID_W
    row = jnp.repeat(jnp.arange(rows_n), GRID_W).astype(jnp.float32)
    col = jnp.tile(jnp.arange(GRID_W), rows_n).astype(jnp.float32)
    nf = QK_ROPE // 4
    inv = ROPE_BASE ** (-jnp.arange(nf, dtype=jnp.float32) / nf)
    ang = jnp.stack([row[:, None] * inv, col[:, None] * inv], axis=1)
    return jnp.cos(ang), jnp.sin(ang)


def apply_rope(x, cos, sin):
    xr = x.reshape(x.shape[:-1] + (2, 2, QK_ROPE // 4))
    x1, x2 = xr[..., 0, :], xr[..., 1, :]
    cos = cos.astype(x.dtype)
    sin = sin.astype(x.dtype)
    out = jnp.stack([x1 * cos - x2 * sin, x2 * cos + x1 * sin], axis=-2)
    return out.reshape(x.shape)


def mla_q(q_lat, p):
    cq = rms_norm(q_lat, p['g_qa'])
    q = (cq @ p['w_uq']).reshape(q_lat.shape[:-1] + (MLA_HEADS, QK_NOPE + QK_ROPE))
    return q[..., :QK_NOPE], q[..., QK_NOPE:]


def mla_kv(kv_lat, p):
    ckv = rms_norm(kv_lat, p['g_kva'])
    kv = (ckv @ p['w_ukv']).reshape(kv_lat.shape[:-1] + (MLA_HEADS, QK_NOPE + V_DIM))
    return kv[..., :QK_NOPE], kv[..., QK_NOPE:]


def mla_attend(q_nope, q_pe, k_nope, k_pe, v):
    s = (jnp.einsum('bqhd,bkhd->bhqk', q_nope, k_nope)
         + jnp.einsum('bqhr,bkr->bhqk', q_pe, k_pe))
    prob = jax.nn.softmax(s.astype(jnp.float32) * MLA_SCALE, axis=-1).astype(v.dtype)
    return jnp.einsum('bhqk,bkhd->bqhd', prob, v)


def mla_blocked(q_nope, q_pe, k_nope, k_pe, v):
    b, s = q_nope.shape[:2]
    nb = s // Q_BLOCK

    def blocks(t):
        return jnp.swapaxes(t.reshape((b, nb, Q_BLOCK) + t.shape[2:]), 0, 1)

    out = lax.map(lambda qs: mla_attend(qs[0], qs[1], k_nope, k_pe, v),
                  (blocks(q_nope), blocks(q_pe)))
    return jnp.swapaxes(out, 0, 1).reshape(b, s, MLA_HEADS * V_DIM)


def segsum(a):
    t = a.shape[-1]
    cs = jnp.cumsum(a, axis=-1)
    diff = cs[..., :, None] - cs[..., None, :]
    return jnp.where(jnp.tril(jnp.ones((t, t), dtype=bool)), diff, -jnp.inf)


def ssd_scan(x, dt, a, bm, cm, h0):
    b, l, h, pd = x.shape
    g, n = bm.shape[2:]
    r = h // g
    q = SSM_CHUNK
    c = l // q
    f32 = jnp.float32
    xq = x.astype(f32).reshape(b, c, q, g, r, pd)
    dtq = dt.astype(f32).reshape(b, c, q, g, r)
    bq = bm.astype(f32).reshape(b, c, q, g, n)
    cq = cm.astype(f32).reshape(b, c, q, g, n)
    xdt = xq * dtq[..., None]
    adt = jnp.transpose(dtq * a.reshape(g, r), (0, 3, 4, 1, 2))
    a_cs = jnp.cumsum(adt, axis=-1)
    lmat = jnp.exp(segsum(adt))
    cb = jnp.einsum('bclgn,bcsgn->bgcls', cq, bq)
    y_diag = jnp.einsum('bgcls,bgrcls,bcsgrp->bclgrp', cb, lmat, xdt)
    decay_s = jnp.exp(a_cs[..., -1:] - a_cs)
    states = jnp.einsum('bcsgn,bgrcs,bcsgrp->bcgrpn', bq, decay_s, xdt)
    states = jnp.concatenate([h0.astype(f32)[:, None], states], axis=1)
    chunk_a = jnp.pad(a_cs[..., -1], ((0, 0), (0, 0), (0, 0), (1, 0)))
    decay_c = jnp.exp(segsum(chunk_a))
    states = jnp.einsum('bgrzc,bcgrpn->bzgrpn', decay_c, states)
    prev, final = states[:, :-1], states[:, -1]
    y_off = jnp.einsum('bclgn,bcgrpn,bgrcl->bclgrp', cq, prev, jnp.exp(a_cs))
    y = (y_diag + y_off).reshape(b, l, h, pd)
    return y.astype(x.dtype), final


def ssm_inputs(xs, bs, cs, dt_raw, p):
    xbc = jax.nn.silu(dwconv(jnp.concatenate([xs, bs, cs], axis=-1), p['w_ssm_conv'], p['b_ssm_conv']))
    xh, bm, cm = jnp.split(xbc, [SSM_INNER, SSM_INNER + SSM_GROUPS * SSM_STATE], axis=-1)
    lead = xs.shape[:2]
    xh = xh.reshape(lead + (SSM_HEADS, SSM_HEADDIM))
    bm = bm.reshape(lead + (SSM_GROUPS, SSM_STATE))
    cm = cm.reshape(lead + (SSM_GROUPS, SSM_STATE))
    dt_f = jax.nn.softplus(dt_raw[..., :SSM_HEADS] + p['dt_bias_f'])
    dt_b = jax.nn.softplus(dt_raw[..., SSM_HEADS:] + p['dt_bias_b'])
    return xh, bm, cm, dt_f, dt_b


def ssm_bidir(inp, h0_f, h0_b, p):
    xh, bm, cm, dt_f, dt_b = inp
    a_f = -jnp.exp(p['a_log_f'].astype(jnp.float32))
    a_b = -jnp.exp(p['a_log_b'].astype(jnp.float32))
    y_f, h_f = ssd_scan(xh, dt_f, a_f, bm, cm, h0_f)

    def fl(t):
        return jnp.flip(t, axis=1)

    y_b, h_b = ssd_scan(fl(xh), fl(dt_b), a_b, fl(bm), fl(cm), h0_b)
    y = y_f + fl(y_b) + p['d_skip'][:, None] * xh
    return y, h_f, h_b


def ssm_out(y, z, p):
    y = y.reshape(z.shape)
    return rms_norm(y * jax.nn.silu(z), p['g_ssm_norm']) @ p['w_ssm_out']


def merge_branches(gates, ya, yb, yc, w_out):
    g = jax.nn.sigmoid(gates.reshape(gates.shape[:-1] + (N_BRANCH, D_MODEL)))
    return (g[..., 0, :] * ya + g[..., 1, :] * yb + g[..., 2, :] * yc) @ w_out


def token_mix(u_x, u_c, p, cos, sin, need_ctx):
    (sb_x, sc_x, sh_x, ql_x, kvl_x, kr_x, z_x, xs_x, bs_x, cs_x, dt_x, gt_x) = jnp.split(
        u_x @ p['w_in'], SPLIT_POINTS, axis=-1)
    (sb_c, sc_c, sh_c, ql_c, kvl_c, kr_c, z_c, xs_c, bs_c, cs_c, dt_c, gt_c) = jnp.split(
        u_c @ p['w_in'], SPLIT_POINTS, axis=-1)
    b = u_x.shape[0]
    ya_x = (sb_x * dwconv(sc_x * sh_x, p['w_sc_conv'])) @ p['w_sc_out']
    kn_c, v_c = mla_kv(kvl_c, p)
    kn_x, v_x = mla_kv(kvl_x, p)
    qn_x, qp_x = mla_q(ql_x, p)
    qp_x = apply_rope(qp_x, cos[:, None], sin[:, None])
    kr_xr = apply_rope(kr_x, cos, sin)
    o_x = mla_blocked(qn_x, qp_x,
                      jnp.concatenate([kn_c, kn_x], axis=1),
                      jnp.concatenate([kr_c, kr_xr], axis=1),
                      jnp.concatenate([v_c, v_x], axis=1))
    yb_x = o_x @ p['w_mla_out']
    h0 = jnp.zeros((b, SSM_GROUPS, SSM_HEADS // SSM_GROUPS, SSM_HEADDIM, SSM_STATE), jnp.float32)
    y_ssm_c, h_f, h_b = ssm_bidir(ssm_inputs(xs_c, bs_c, cs_c, dt_c, p), h0, h0, p)
    y_ssm_x, _, _ = ssm_bidir(ssm_inputs(xs_x, bs_x, cs_x, dt_x, p), h_f, h_b, p)
    yc_x = ssm_out(y_ssm_x, z_x, p)
    out_x = merge_branches(gt_x, ya_x, yb_x, yc_x, p['w_out'])
    if not need_ctx:
        return out_x, None
    ya_c = (sb_c * dwconv(sc_c * sh_c, p['w_sc_conv'])) @ p['w_sc_out']
    qn_c, qp_c = mla_q(ql_c, p)
    o_c = mla_attend(qn_c, qp_c, kn_c, kr_c, v_c)
    yb_c = o_c.reshape(o_c.shape[:2] + (MLA_HEADS * V_DIM,)) @ p['w_mla_out']
    yc_c = ssm_out(y_ssm_c, z_c, p)
    out_c = merge_branches(gt_c, ya_c, yb_c, yc_c, p['w_out'])
    return out_x, out_c


def conv_ffn(u, p):
    gate, val = jnp.split(u @ p['w_up'], 2, axis=-1)
    return (jax.nn.silu(dwconv(gate, p['w_ff_conv'])) * val) @ p['w_down']


def setup_inputs(seed: int = 0) -> dict:
    key = jax.random.key(seed)
    keys = jax.random.split(key, 32)
    f32 = jnp.float32
    L = DEPTH

    def nrm(i, shape, scale):
        return jax.random.normal(keys[i], shape, f32) * scale

    def gain(i, shape):
        return 1.0 + nrm(i, shape, 0.01)

    def dt_bias(i):
        dt0 = jnp.exp(jax.random.uniform(keys[i], (L, SSM_HEADS), f32,
                                         minval=math.log(1e-3), maxval=math.log(1e-1)))
        return dt0 + jnp.log(-jnp.expm1(-dt0))

    def a_log(i):
        return jnp.log(jax.random.uniform(keys[i], (L, SSM_HEADS), f32, minval=1.0, maxval=16.0))

    return {
        'x': nrm(0, (BATCH, SEQ, D_MODEL), 1.0),
        'c': nrm(1, (BATCH, D_MODEL), 1.0),
        'ctx': nrm(2, (BATCH, CTX_LEN, D_MODEL), 1.0),
        'c_ctx': nrm(3, (D_MODEL,), 1.0),
        'w_ada': nrm(4, (L, D_MODEL, 6 * D_MODEL), 0.5 * D_MODEL ** -0.5),
        'b_ada': nrm(5, (L, 6 * D_MODEL), 0.01),
        'w_in': nrm(6, (L, D_MODEL, D_IN), D_MODEL ** -0.5),
        'w_sc_conv': nrm(7, (L, CONV_K, SC_WIDTH), CONV_K ** -0.5),
        'w_sc_out': nrm(8, (L, SC_WIDTH, D_MODEL), SC_WIDTH ** -0.5),
        'g_qa': gain(9, (L, Q_LORA)),
        'w_uq': nrm(10, (L, Q_LORA, MLA_HEADS * (QK_NOPE + QK_ROPE)), Q_LORA ** -0.5),
        'g_kva': gain(11, (L, KV_LORA)),
        'w_ukv': nrm(12, (L, KV_LORA, MLA_HEADS * (QK_NOPE + V_DIM)), KV_LORA ** -0.5),
        'w_mla_out': nrm(13, (L, MLA_HEADS * V_DIM, D_MODEL), (MLA_HEADS * V_DIM) ** -0.5),
        'w_ssm_conv': nrm(14, (L, CONV_K, SSM_CONV_CH), CONV_K ** -0.5),
        'b_ssm_conv': nrm(15, (L, SSM_CONV_CH), 0.01),
        'dt_bias_f': dt_bias(16),
        'dt_bias_b': dt_bias(17),
        'a_log_f': a_log(18),
        'a_log_b': a_log(19),
        'd_skip': gain(20, (L, SSM_HEADS)),
        'g_ssm_norm': gain(21, (L, SSM_INNER)),
        'w_ssm_out': nrm(22, (L, SSM_INNER, D_MODEL), SSM_INNER ** -0.5),
        'w_out': nrm(23, (L, D_MODEL, D_MODEL), BETA * D_MODEL ** -0.5),
        'ln1_g': gain(24, (L, D_MODEL)),
        'ln1_b': nrm(25, (L, D_MODEL), 0.01),
        'w_up': nrm(26, (L, D_MODEL, 2 * D_FF), D_MODEL ** -0.5),
        'w_ff_conv': nrm(27, (L, CONV_K, D_FF), CONV_K ** -0.5),
        'w_down': nrm(28, (L, D_FF, D_MODEL), BETA * D_FF ** -0.5),
        'ln2_g': gain(29, (L, D_MODEL)),
        'ln2_b': nrm(30, (L, D_MODEL), 0.01),
    }


def reference(x, c, ctx, c_ctx, w_ada, b_ada, w_in, w_sc_conv, w_sc_out, g_qa, w_uq, g_kva, w_ukv,
              w_mla_out, w_ssm_conv, b_ssm_conv, dt_bias_f, dt_bias_b, a_log_f, a_log_b, d_skip,
              g_ssm_norm, w_ssm_out, w_out, ln1_g, ln1_b, w_up, w_ff_conv, w_down, ln2_g, ln2_b):
    cos, sin = axial_rope_tables(x.shape[1])
    sc = jax.nn.silu(c)
    scc = jax.nn.silu(c_ctx)
    for l in range(DEPTH):
        need_ctx = l < DEPTH - 1
        p = {
            'w_in': w_in[l], 'w_sc_conv': w_sc_conv[l], 'w_sc_out': w_sc_out[l],
            'g_qa': g_qa[l], 'w_uq': w_uq[l], 'g_kva': g_kva[l], 'w_ukv': w_ukv[l],
            'w_mla_out': w_mla_out[l], 'w_ssm_conv': w_ssm_conv[l], 'b_ssm_conv': b_ssm_conv[l],
            'dt_bias_f': dt_bias_f[l], 'dt_bias_b': dt_bias_b[l], 'a_log_f': a_log_f[l],
            'a_log_b': a_log_b[l], 'd_skip': d_skip[l], 'g_ssm_norm': g_ssm_norm[l],
            'w_ssm_out': w_ssm_out[l], 'w_out': w_out[l], 'w_up': w_up[l],
            'w_ff_conv': w_ff_conv[l], 'w_down': w_down[l],
        }
        mx = jnp.split((sc @ w_ada[l] + b_ada[l])[:, None, :], 6, axis=-1)
        mc = jnp.split(scc @ w_ada[l] + b_ada[l], 6, axis=-1)
        y_x, y_c = token_mix(modulate(x, mx[0], mx[1]), modulate(ctx, mc[0], mc[1]),
                             p, cos, sin, need_ctx)
        x = post_norm(ALPHA * x + mx[2] * y_x, ln1_g[l], ln1_b[l])
        x = post_norm(ALPHA * x + mx[5] * conv_ffn(modulate(x, mx[3], mx[4]), p), ln2_g[l], ln2_b[l])
        if need_ctx:
            ctx = post_norm(ALPHA * ctx + mc[2] * y_c, ln1_g[l], ln1_b[l])
            ctx = post_norm(ALPHA * ctx + mc[5] * conv_ffn(modulate(ctx, mc[3], mc[4]), p),
                            ln2_g[l], ln2_b[l])
    return x
```

```python
import math
from contextlib import ExitStack
import numpy as np
import concourse.bass as bass
import concourse.mybir as mybir
from concourse.bass_utils import run_bass_kernel_spmd

F32 = mybir.dt.float32
BF16 = mybir.dt.bfloat16
AF = mybir.ActivationFunctionType
ALU = mybir.AluOpType
NDS = 24
SAME_SYNC = True
STQ = 'act'
NEG = -30000.0
CC_MAXBYTES = 1 << 20


class Cfg:
    def __init__(s, D=2048, S=8192, DEPTH=4, GRID_W=64, CTX=256, H=16, QL=512, KVL=512, SI=4096, SG=8,
                 FF=5632, B=2, TP=4):
        s.D, s.S, s.DEPTH, s.GRID_W, s.CTX, s.H, s.QL, s.KVL, s.SI, s.SG, s.FF, s.B, s.TP = \
            D, S, DEPTH, GRID_W, CTX, H, QL, KVL, SI, SG, FF, B, TP
        s.SH = SI // 64
        assert s.SH // SG == 8
        s.T = CTX + S
        s.DC = D // 128
        s.QLC, s.KVC, s.SIC, s.FFC = QL // 128, KVL // 128, SI // 128, FF // 128
        s.Dl, s.DCl = D // TP, s.DC // TP
        s.Hl = H // TP
        s.SIl, s.SICl, s.SHl, s.SGl = SI // TP, s.SIC // TP, s.SH // TP, SG // TP
        s.FFl, s.FFCl = FF // TP, s.FFC // TP
        assert s.DCl * TP == s.DC and s.Hl * TP == H and s.Hl % 2 == 0 and s.SGl * TP == SG and s.FFCl * TP == s.FFC
        assert s.SHl <= 64
        s.CCl = s.SIl + 2 * s.SGl * 128
        s.CCCl = s.CCl // 128
        s.CC = SI + 2 * SG * 128
        s.ALPHA = (2 * DEPTH) ** 0.25
        s.EPS = 1e-6
        s.SCALE = (128 + 64) ** -0.5
        s.NJIN = 3 * s.DCl + s.QLC + s.KVC + 2 + s.SICl + s.CCCl + 1 + 3 * s.DCl
        o = 0
        s.V = {}
        for nm, n in (("scconv", s.DCl * 3), ("gqa", s.QLC), ("gkva", s.KVC), ("ssmconv", s.CCCl * 3),
                      ("ssmb", s.CCCl), ("gssm", s.SIC), ("dsk", s.SICl), ("ln1g", s.DC), ("ln1b", s.DC),
                      ("ln2g", s.DC), ("ln2b", s.DC), ("ffconv", s.FFCl * 3)):
            s.V[nm] = o
            o += n
        s.NV = o
        s.tiles = [(0, CTX, 1)] + [(CTX + 512 * i, 512, 0) for i in range(S // 512)]

    def seq(s, kind):
        return (0, s.CTX) if kind else (s.CTX, s.T)


class Buf:
    __slots__ = ("wr", "rd")

    def __init__(s):
        s.wr = None
        s.rd = {}


class Stage:
    def __init__(s, k, name):
        s.k, s.name, s.es = k, name, ExitStack()

    def sb(s, shape, dt):
        s.k.uid += 1
        t = s.es.enter_context(s.k.nc.sbuf_tensor(f"{s.name}_{s.k.uid}", list(shape), dt))
        return t, Buf()

    def ps(s, shape=(128, 512), dt=F32):
        s.k.uid += 1
        t = s.es.enter_context(s.k.nc.psum_tensor(f"{s.name}p_{s.k.uid}", list(shape), dt))
        return t, Buf()

    def __enter__(s):
        return s

    def __exit__(s, *a):
        if a[0] is None:
            s.k.barrier()
        s.es.close()
        return False


class K:
    def __init__(s, nc):
        s.nc = nc
        s.eng = {'pe': nc.tensor, 'act': nc.scalar, 'dve': nc.vector, 'pool': nc.gpsimd, 'sp': nc.sync}
        s.sem = {e: nc.alloc_semaphore(name=f"s_{e}") for e in s.eng}
        s.cnt = {e: 0 for e in s.eng}
        s.waited = {e: {} for e in s.eng}
        s.dsems = [nc.alloc_semaphore(name=f"dq{i}") for i in range(NDS)]
        s.dval = [0] * NDS
        s.dnext = 0
        s.uid = 0
        s.ninstr = 0
        s.ctoks = []

    def stage(s, name):
        return Stage(s, name)

    def _wait(s, e, tok):
        key, val, sem = tok
        if key == e and not SAME_SYNC:
            return
        if s.waited[e].get(key, 0) >= val:
            return
        s.eng[e].wait_ge(sem, val)
        s.waited[e][key] = val
        s.ninstr += 1

    def _deps(s, e, r, w, acc):
        for b in r:
            if b.wr is not None:
                s._wait(e, b.wr)
        for b in w:
            if b.wr is not None and not (acc and b.wr[0] == e):
                s._wait(e, b.wr)
            for tok in b.rd.values():
                s._wait(e, tok)

    def _post(s, tok, r, w):
        key = tok[0]
        for b in r:
            b.rd[key] = tok
        for b in w:
            b.wr = tok
            b.rd = {}

    def op(s, e, fn, r=(), w=(), acc=False):
        s._deps(e, r, w, acc)
        ins = fn(s.eng[e])
        s.cnt[e] += 1
        ins.then_inc(s.sem[e], 1)
        tok = (e, s.cnt[e], s.sem[e])
        s._post(tok, r, w)
        s.ninstr += 1
        return tok

    def dma(s, e, out, in_, r=(), w=()):
        s._deps(e, r, w, False)
        i = s.dnext
        s.dnext = (i + 1) % NDS
        if s.dval[i] > 0:
            s._wait(e, (('d', i), s.dval[i], s.dsems[i]))
        ins = s.eng[e].dma_start(out=out, in_=in_)
        s.dval[i] += 16
        ins.then_inc(s.dsems[i], 16)
        tok = (('d', i), s.dval[i], s.dsems[i])
        s._post(tok, r, w)
        s.ninstr += 1
        return tok

    def allgather(s, cfg, src, dst):
        s.barrier()
        nc = s.nc
        groups = [[b * cfg.TP + r for r in range(cfg.TP)] for b in range(cfg.B)]
        R, T = src.shape
        TP = cfg.TP
        wmax = max(64, min(T, (CC_MAXBYTES // (R * 2)) // 64 * 64))
        chunks = [(t0, min(wmax, T - t0)) for t0 in range(0, T, wmax)]
        if not hasattr(s, 'stg'):
            s.stg = {}
            s.csem = nc.alloc_semaphore(name="ccsem")
            s.ccount = 0
        NSL = 3
        for i, (t0, cw) in enumerate(chunks):
            key = (R, cw, i % NSL)
            if key not in s.stg:
                s.uid += 1
                tin = nc.dram_tensor(f"stin{s.uid}", [R, cw], BF16).ap()
                tout = nc.dram_tensor(f"stout{s.uid}", [TP * R, cw], BF16).ap()
                s.stg[key] = (tin, Buf(), tout, Buf())
            tin, tinb, tout, toutb = s.stg[key]
            s.dma('pool', tin, src[:, t0:t0 + cw], w=[tinb])
            s._deps('pool', [tinb], [toutb], False)
            ins = nc.gpsimd.collective_compute("AllGather", ALU.bypass, replica_groups=groups,
                                               ins=[tin.opt()], outs=[tout.opt()])
            s.ccount += 1
            ins.then_inc(s.csem)
            tok = ('cc', s.ccount, s.csem)
            s._post(tok, [tinb], [toutb])
            s.ninstr += 1
            s.dma('sp', dst[:, t0:t0 + cw], tout, r=[toutb])
        s.barrier()

    def barrier(s):
        toks = [(e, s.cnt[e], s.sem[e]) for e in s.eng if s.cnt[e] > 0]
        if getattr(s, 'ccount', 0) > 0:
            toks.append(('cc', s.ccount, s.csem))
        toks += [(('d', i), s.dval[i], s.dsems[i]) for i in range(NDS) if s.dval[i] > 0]
        for e in s.eng:
            for t in toks:
                s._wait(e, t)


def fm(ap):
    return ap.rearrange("(c p) t -> p c t", p=128)


def split_groups(items, widths, maxw):
    n = len(items)
    tot = sum(widths)
    g = max(1, -(-tot // maxw))
    while True:
        idx = np.array_split(np.arange(n), g)
        if all(sum(widths[i] for i in ix) <= maxw for ix in idx):
            return [[items[i] for i in ix] for ix in idx]
        g += 1


def linear(k, cfg, name, src, KC, wq, NJ, dstf, budget=90 * 1024):
    PC = 16
    npiece = -(-KC // PC)
    maxw = max(512, min(2048, (budget // (KC * 2)) // 512 * 512))
    tiles = cfg.tiles
    groups = split_groups(tiles, [t[1] for t in tiles], maxw)
    gw = max(sum(t[1] for t in g) for g in groups)
    srcv = fm(src)
    with k.stage(name) as st:
        acts, actsb = st.sb([128, KC, gw], BF16)
        NW = 3
        w32 = [st.sb([128, PC, 128], F32) for _ in range(NW)]
        wbf = [st.sb([128, PC, 128], BF16) for _ in range(NW + 1)]
        obf = [st.sb([128, gw], BF16) for _ in range(3)]
        of32 = [st.sb([128, gw], F32) for _ in range(2)]
        banks = [st.ps() for _ in range(8)]
        jobs = [(j, pi) for j in range(NJ) for pi in range(npiece)]
        wcount = [0]

        def issue_load(idx):
            j, pi = jobs[idx]
            k0, k1 = pi * PC, min(KC, (pi + 1) * PC)
            wt, wtb = w32[idx % NW]
            k.dma('sp', wt[:, 0:k1 - k0, :], wq[j, :, k0:k1, :], w=[wtb])

        jc = 0
        for g in groups:
            t0 = g[0][0]
            W = sum(t[1] for t in g)
            for pi in range(npiece):
                k0, k1 = pi * PC, min(KC, (pi + 1) * PC)
                k.dma('sp', acts[:, k0:k1, 0:W], srcv[:, k0:k1, t0:t0 + W], w=[actsb])
            PF = 2
            for idx in range(min(PF, len(jobs))):
                issue_load(idx)
            for idx, (j, pi) in enumerate(jobs):
                if idx + PF < len(jobs):
                    issue_load(idx + PF)
                k0, k1 = pi * PC, min(KC, (pi + 1) * PC)
                wt, wtb = w32[idx % NW]
                wb, wbb = wbf[idx % (NW + 1)]
                k.op('pool', lambda E: E.tensor_copy(out=wb[:, 0:k1 - k0, :], in_=wt[:, 0:k1 - k0, :]),
                     r=[wtb], w=[wbb])
                bset = (jc % 2) * 4
                for kc in range(k0, k1):
                    off = 0
                    for ti, (tt0, tw, kind) in enumerate(g):
                        ps, psb = banks[bset + ti]
                        k.op('pe', lambda E: E.matmul(ps[:, 0:tw], wb[:, kc - k0, :], acts[:, kc, off:off + tw],
                                                      start=(kc == 0), stop=(kc == KC - 1)),
                             r=[wbb, actsb], w=[psb], acc=True)
                        off += tw
                if pi == npiece - 1:
                    dst, kind = dstf(j)
                    isf = (dst.dtype == F32)
                    ot, otb = (of32[jc % 2] if isf else obf[jc % 3])
                    off = 0
                    for ti, (tt0, tw, _) in enumerate(g):
                        ps, psb = banks[bset + ti]
                        if kind == 'copy':
                            if (jc + ti) % 2 == 0:
                                k.op('act', lambda E: E.activation(out=ot[:, off:off + tw], in_=ps[:, 0:tw],
                                                                   func=AF.Copy), r=[psb], w=[otb])
                            else:
                                k.op('dve', lambda E: E.tensor_copy(out=ot[:, off:off + tw], in_=ps[:, 0:tw]),
                                     r=[psb], w=[otb])
                        else:
                            fn = AF.Silu if kind == 'silu' else AF.Sigmoid
                            k.op('act', lambda E: E.activation(out=ot[:, off:off + tw], in_=ps[:, 0:tw], func=fn),
                                 r=[psb], w=[otb])
                        off += tw
                    k.dma(STQ, dst[:, t0:t0 + W], ot[:, 0:W], r=[otb])
                    jc += 1


def ln_apply(k, cfg, R, vt, vtb, w, sc_ap, bi_ap, ut, utb, nfeat_chunks):
    C = nfeat_chunks
    sq, sqb = R['sq']
    ps1, ps1b = R['ps1']
    ps2, ps2b = R['ps2']
    ones = R['ones']
    onesb = R['onesb']
    inv = 1.0 / (C * 128)
    k.op('act', lambda E: E.activation(out=sq[:, 0:C, 0:w], in_=vt[:, 0:C, 0:w], func=AF.Square), r=[vtb], w=[sqb])
    for c in range(C):
        k.op('pe', lambda E: E.matmul(ps1[:, 0:w], ones, vt[:, c, 0:w], start=(c == 0), stop=(c == C - 1)),
             r=[vtb, onesb], w=[ps1b], acc=True)
    for c in range(C):
        k.op('pe', lambda E: E.matmul(ps2[:, 0:w], ones, sq[:, c, 0:w], start=(c == 0), stop=(c == C - 1)),
             r=[sqb, onesb], w=[ps2b], acc=True)
    (m, mb), (v, vb), (rs, rsb), (nm, nmb) = R['small']
    k.op('dve', lambda E: E.tensor_scalar(out=m[:, 0:w], in0=ps1[:, 0:w], scalar1=inv, scalar2=None, op0=ALU.mult),
         r=[ps1b], w=[mb])
    k.op('dve', lambda E: E.tensor_tensor(out=v[:, 0:w], in0=m[:, 0:w], in1=m[:, 0:w], op=ALU.mult), r=[mb], w=[vb])
    k.op('dve', lambda E: E.scalar_tensor_tensor(out=v[:, 0:w], in0=ps2[:, 0:w], scalar=inv, in1=v[:, 0:w],
                                                 op0=ALU.mult, op1=ALU.subtract), r=[ps2b, vb], w=[vb])
    rstd_from_var(k, cfg, R, v, vb, rs, rsb, w)
    k.op('dve', lambda E: E.scalar_tensor_tensor(out=nm[:, 0:w], in0=m[:, 0:w], scalar=-1.0, in1=rs[:, 0:w],
                                                 op0=ALU.mult, op1=ALU.mult), r=[mb, rsb], w=[nmb])
    k.op('dve', lambda E: E.tensor_tensor(out=sq[:, 0:C, 0:w], in0=vt[:, 0:C, 0:w],
                                          in1=rs[:, 0:w].unsqueeze(1).to_broadcast([128, C, w]), op=ALU.mult),
         r=[vtb, rsb], w=[sqb])
    k.op('pool', lambda E: E.tensor_tensor(out=sq[:, 0:C, 0:w], in0=sq[:, 0:C, 0:w],
                                           in1=nm[:, 0:w].unsqueeze(1).to_broadcast([128, C, w]), op=ALU.add),
         r=[sqb, nmb], w=[sqb])
    for c in range(C):
        k.op('act', lambda E: E.activation(out=ut[:, c, 0:w], in_=sq[:, c, 0:w], func=AF.Identity,
                                           bias=bi_ap(c), scale=sc_ap(c)), r=[sqb], w=[utb])


def rstd_from_var(k, cfg, R, v, vb, rs, rsb, w):
    eps = R['eps']
    k.op('act', lambda E: E.activation(out=rs[:, 0:w], in_=v[:, 0:w], func=AF.Ln, bias=eps[:, 0:1], scale=1.0),
         r=[vb], w=[rsb])
    k.op('act', lambda E: E.activation(out=rs[:, 0:w], in_=rs[:, 0:w], func=AF.Exp, scale=-0.5), r=[rsb], w=[rsb])


def ln_res(k, st, G, C=None, wmax=512):
    R = {'sq': st.sb([128, C, wmax], F32), 'ps1': st.ps(), 'ps2': st.ps(), 'ones': G['ones_f'], 'onesb': G['cb'],
         'small': [st.sb([128, wmax], F32) for _ in range(4)], 'eps': G['eps']}
    return R


def st_lnmod(k, cfg, G, X, U, m_shift, m_scale):
    DC = cfg.DC
    mods = G['mods']
    Xv, Uv = fm(X), fm(U)
    with k.stage("lnmod") as st:
        xt = [st.sb([128, DC, 512], F32) for _ in range(2)]
        ut = [st.sb([128, DC, 512], BF16) for _ in range(2)]
        R = ln_res(k, st, G, DC)
        for i, (t0, w, kind) in enumerate(cfg.tiles):
            x, xb = xt[i % 2]
            u, ub = ut[i % 2]
            k.dma('sp', x[:, :, 0:w], Xv[:, :, t0:t0 + w], w=[xb])
            ln_apply(k, cfg, R, x, xb, w,
                     lambda c: mods[:, m_scale * DC + c, kind:kind + 1],
                     lambda c: mods[:, m_shift * DC + c, kind:kind + 1], u, ub, DC)
            k.dma(STQ, Uv[:, :, t0:t0 + w], u[:, :, 0:w], r=[ub])


def st_postnorm(k, cfg, G, X, Y, OUT, m_idx, vg, vb_, l, only_x=False):
    DC = cfg.DC
    mods, vec = G['mods'], G['vec']
    Xv, Yv, Ov = fm(X), fm(Y), fm(OUT)
    with k.stage("postnorm") as st:
        xt = [st.sb([128, DC, 512], F32) for _ in range(2)]
        yt = [st.sb([128, DC, 512], F32) for _ in range(1)]
        ybt = [st.sb([128, DC, 512], BF16) for _ in range(1)]
        R = ln_res(k, st, G, DC)
        for i, (t0, w, kind) in enumerate(cfg.tiles):
            if only_x and kind == 1:
                continue
            x, xb = xt[i % 2]
            y, yb = yt[0]
            k.dma('sp', x[:, :, 0:w], Xv[:, :, t0:t0 + w], w=[xb])
            yq, yqb = ybt[0]
            k.dma('sp', yq[:, :, 0:w], Yv[:, :, t0:t0 + w], w=[yqb])
            k.op('act', lambda E: E.mul(out=x[:, :, 0:w], in_=x[:, :, 0:w], mul=float(cfg.ALPHA)), r=[xb], w=[xb])
            for c in range(DC):
                k.op('dve', lambda E: E.scalar_tensor_tensor(out=y[:, c, 0:w], in0=yq[:, c, 0:w],
                                                             scalar=mods[:, m_idx * DC + c, kind:kind + 1],
                                                             in1=x[:, c, 0:w], op0=ALU.mult, op1=ALU.add),
                     r=[yqb, xb, G['modsb']], w=[yb])
            sq, sqb = R['sq']
            ln_apply(k, cfg, R, y, yb, w,
                     lambda c: vec[:, vg + c:vg + c + 1], lambda c: vec[:, vb_ + c:vb_ + c + 1], sq, sqb, DC)
            o0 = t0 - cfg.CTX if only_x else t0
            k.dma(STQ, Ov[:, :, o0:o0 + w], sq[:, :, 0:w], r=[sqb])


def st_ada(k, cfg, G, A, l):
    DC = cfg.DC
    NJ = 6 * DC
    mods, modsb = G['mods'], G['modsb']
    with k.stage("ada") as st:
        cc, ccb = st.sb([128, DC, 2], F32)
        scs, scsb = st.sb([128, DC, 2], F32)
        bad, badb = st.sb([128, NJ], F32)
        ps, psb = st.ps()
        wbs = [st.sb([128, DC, 128], F32) for _ in range(3)]
        k.dma('sp', cc[:], A['ccol'], w=[ccb])
        k.dma('sp', bad[:], A['b_ada'][l], w=[badb])
        k.op('act', lambda E: E.activation(out=scs[:], in_=cc[:], func=AF.Silu), r=[ccb], w=[scsb])
        for j in range(NJ):
            wt, wtb = wbs[j % 3]
            k.dma('sp', wt[:], A['w_ada'][l, j], w=[wtb])
            for kc in range(DC):
                k.op('pe', lambda E: E.matmul(ps[:, 2 * j:2 * j + 2], wt[:, kc, :], scs[:, kc, :],
                                              start=(kc == 0), stop=(kc == DC - 1)),
                     r=[wtb, scsb], w=[psb], acc=True)
        k.op('dve', lambda E: E.tensor_tensor(out=mods[:], in0=ps[:, 0:2 * NJ].rearrange("p (j t) -> p j t", t=2),
                                              in1=bad[:].unsqueeze(2).to_broadcast([128, NJ, 2]), op=ALU.add),
             r=[psb, badb], w=[modsb])
        for m in (1, 4):
            k.op('dve', lambda E: E.tensor_scalar(out=mods[:, m * DC:(m + 1) * DC, :], in0=mods[:, m * DC:(m + 1) * DC, :],
                                                  scalar1=1.0, scalar2=None, op0=ALU.add), r=[modsb], w=[modsb])
        vec, vecb = G['vec'], G['vecb']
        k.dma('sp', vec[:], A['vec'][l], w=[vecb])
        rowt, rowb = G['rows'], G['rowsb']
        for i in range(2):
            k.dma('sp', rowt[:, i, :], A['rowv'][l, i].partition_broadcast(128), w=[rowb])
        k.op('act', lambda E: E.activation(out=rowt[:, 1, :], in_=rowt[:, 1, :], func=AF.Exp), r=[rowb], w=[rowb])
        k.op('dve', lambda E: E.tensor_scalar(out=rowt[:, 1, :], in0=rowt[:, 1, :], scalar1=-1.0, scalar2=None,
                                              op0=ALU.mult), r=[rowb], w=[rowb])


def load_halo(k, q, dst, dstb, src, t0, w, s0, s1):
    lo, hi = max(t0 - 1, s0), min(t0 + w + 1, s1)
    if lo > t0 - 1:
        k.op('pool', lambda E: E.memset(dst[:, 0:1], 0.0), w=[dstb])
    if hi < t0 + w + 1:
        k.op('pool', lambda E: E.memset(dst[:, w + 1:w + 2], 0.0), w=[dstb])
    k.dma(q, dst[:, lo - (t0 - 1):hi - (t0 - 1)], src[:, lo:hi], w=[dstb])


def conv3(k, e, out, outb, p, pb, w, wts, extra_r=()):
    k.op(e, lambda E: E.tensor_scalar(out=out[:, 0:w], in0=p[:, 0:w], scalar1=wts[:, 0:1], scalar2=None, op0=ALU.mult),
         r=[pb, *extra_r], w=[outb])
    for i in (1, 2):
        k.op(e, lambda E: E.scalar_tensor_tensor(out=out[:, 0:w], in0=p[:, i:i + w], scalar=wts[:, i:i + 1],
                                                 in1=out[:, 0:w], op0=ALU.mult, op1=ALU.add),
             r=[pb, outb, *extra_r], w=[outb])


def st_mixa(k, cfg, G, ABC, AIN):
    DC, D = cfg.DCl, cfg.Dl
    vec = G['vec']
    with k.stage("mixa") as st:
        p1 = [st.sb([128, 514], BF16) for _ in range(2)]
        p2 = [st.sb([128, 514], BF16) for _ in range(2)]
        p3 = [st.sb([128, 512], BF16) for _ in range(2)]
        pr = [st.sb([128, 514], F32) for _ in range(2)]
        cv = [st.sb([128, 512], F32) for _ in range(2)]
        ob = [st.sb([128, 512], BF16) for _ in range(2)]
        n = 0
        for c in range(DC):
            wts = vec[:, cfg.V['scconv'] + 3 * c: cfg.V['scconv'] + 3 * c + 3]
            for (t0, w, kind) in cfg.tiles:
                s0, s1 = cfg.seq(kind)
                (a, ab), (b, bb), (s, sb_), (p, pb), (cvt, cvb), (o, obb) = p1[n % 2], p2[n % 2], p3[n % 2], pr[n % 2], cv[n % 2], ob[n % 2]
                load_halo(k, 'sp', a, ab, ABC[D + c * 128:D + (c + 1) * 128, :], t0, w, s0, s1)
                load_halo(k, 'sp', b, bb, ABC[2 * D + c * 128:2 * D + (c + 1) * 128, :], t0, w, s0, s1)
                k.dma('sp', s[:, 0:w], ABC[c * 128:(c + 1) * 128, t0:t0 + w], w=[sb_])
                k.op('dve', lambda E: E.tensor_tensor(out=p[:, 0:w + 2], in0=a[:, 0:w + 2], in1=b[:, 0:w + 2], op=ALU.mult),
                     r=[ab, bb], w=[pb])
                conv3(k, 'dve', cvt, cvb, p, pb, w, wts, [G['vecb']])
                k.op('pool' if n % 2 == 0 else 'dve',
                     lambda E: E.tensor_tensor(out=o[:, 0:w], in0=cvt[:, 0:w], in1=s[:, 0:w], op=ALU.mult),
                     r=[cvb, sb_], w=[obb])
                k.dma(STQ, AIN[c * 128:(c + 1) * 128, t0:t0 + w], o[:, 0:w], r=[obb])
                n += 1


def st_ffconv(k, cfg, G, GU, HH):
    FFC, FF = cfg.FFCl, cfg.FFl
    vec = G['vec']
    with k.stage("ffconv") as st:
        p1 = [st.sb([128, 514], BF16) for _ in range(2)]
        p3 = [st.sb([128, 512], BF16) for _ in range(2)]
        cv = [st.sb([128, 512], F32) for _ in range(2)]
        ob = [st.sb([128, 512], BF16) for _ in range(2)]
        n = 0
        for c in range(FFC):
            wts = vec[:, cfg.V['ffconv'] + 3 * c: cfg.V['ffconv'] + 3 * c + 3]
            for (t0, w, kind) in cfg.tiles:
                s0, s1 = cfg.seq(kind)
                (a, ab), (s, sb_), (cvt, cvb), (o, obb) = p1[n % 2], p3[n % 2], cv[n % 2], ob[n % 2]
                load_halo(k, 'sp', a, ab, GU[c * 128:(c + 1) * 128, :], t0, w, s0, s1)
                k.dma('sp', s[:, 0:w], GU[FF + c * 128:FF + (c + 1) * 128, t0:t0 + w], w=[sb_])
                conv3(k, 'dve', cvt, cvb, a, ab, w, wts, [G['vecb']])
                k.op('act', lambda E: E.activation(out=cvt[:, 0:w], in_=cvt[:, 0:w], func=AF.Silu), r=[cvb], w=[cvb])
                k.op('pool' if n % 2 == 0 else 'dve',
                     lambda E: E.tensor_tensor(out=o[:, 0:w], in0=cvt[:, 0:w], in1=s[:, 0:w], op=ALU.mult),
                     r=[cvb, sb_], w=[obb])
                k.dma(STQ, HH[c * 128:(c + 1) * 128, t0:t0 + w], o[:, 0:w], r=[obb])
                n += 1


def st_merge(k, cfg, G, GT, YA, YB, YC, MG):
    DC, D = cfg.DCl, cfg.Dl
    with k.stage("merge") as st:
        gt = [st.sb([128, 3, 512], BF16) for _ in range(2)]
        ys = [st.sb([128, 3, 512], BF16) for _ in range(2)]
        pr = [st.sb([128, 3, 512], F32) for _ in range(2)]
        ob = [st.sb([128, 512], BF16) for _ in range(2)]
        GTv = GT.rearrange("(m r) t -> r m t", m=3)
        n = 0
        for c in range(DC):
            for (t0, w, kind) in cfg.tiles:
                (g, gb), (y, yb), (p, pb), (o, obb) = gt[n % 2], ys[n % 2], pr[n % 2], ob[n % 2]
                k.dma('sp', g[:, :, 0:w], GTv[c * 128:(c + 1) * 128, :, t0:t0 + w], w=[gb])
                for i, Y in enumerate((YA, YB, YC)):
                    k.dma('sp', y[:, i, 0:w], Y[c * 128:(c + 1) * 128, t0:t0 + w], w=[yb])
                e1, e2 = ('dve', 'pool') if n % 2 == 0 else ('pool', 'dve')
                k.op(e1, lambda E: E.tensor_tensor(out=p[:, :, 0:w], in0=g[:, :, 0:w], in1=y[:, :, 0:w], op=ALU.mult),
                     r=[gb, yb], w=[pb])
                k.op(e2, lambda E: E.tensor_tensor(out=p[:, 0, 0:w], in0=p[:, 0, 0:w], in1=p[:, 1, 0:w], op=ALU.add),
                     r=[pb], w=[pb])
                k.op(e2, lambda E: E.tensor_tensor(out=o[:, 0:w], in0=p[:, 0, 0:w], in1=p[:, 2, 0:w], op=ALU.add),
                     r=[pb], w=[obb])
                k.dma(STQ, MG[c * 128:(c + 1) * 128, t0:t0 + w], o[:, 0:w], r=[obb])
                n += 1


def st_rms(k, cfg, G, SRC, C, DST, goff, W=512):
    vec = G['vec']
    Sv, Dv = fm(SRC), fm(DST)
    inv = 1.0 / (C * 128)
    with k.stage("rms") as st:
        xt = [st.sb([128, C, W], BF16) for _ in range(2)]
        sq = [st.sb([128, C, W], F32) for _ in range(2)]
        ot = [st.sb([128, C, W], BF16) for _ in range(2)]
        sm = [st.sb([128, W], F32) for _ in range(2)]
        pss = [st.ps() for _ in range(2)]
        for i, (t0, w) in enumerate([(t, min(W, cfg.T - t)) for t in range(0, cfg.T, W)]):
            (x, xb), (s, sb_), (o, obb), (r, rb), (ps, psb) = xt[i % 2], sq[i % 2], ot[i % 2], sm[i % 2], pss[i % 2]
            k.dma('sp', x[:, :, 0:w], Sv[:, :, t0:t0 + w], w=[xb])
            k.op('act', lambda E: E.activation(out=s[:, :, 0:w], in_=x[:, :, 0:w], func=AF.Square), r=[xb], w=[sb_])
            for c in range(C):
                k.op('pe', lambda E: E.matmul(ps[:, 0:w], G['ones_f'], s[:, c, 0:w], start=(c == 0), stop=(c == C - 1)),
                     r=[sb_, G['cb']], w=[psb], acc=True)
            k.op('dve', lambda E: E.tensor_scalar(out=r[:, 0:w], in0=ps[:, 0:w], scalar1=inv, scalar2=None, op0=ALU.mult),
                 r=[psb], w=[rb])
            rstd_from_var(k, cfg, {'eps': G['eps']}, r, rb, r, rb, w)
            k.op('dve', lambda E: E.tensor_tensor(out=s[:, :, 0:w], in0=x[:, :, 0:w],
                                                  in1=r[:, 0:w].unsqueeze(1).to_broadcast([128, C, w]), op=ALU.mult),
                 r=[xb, rb], w=[sb_])
            for c in range(C):
                k.op('act', lambda E: E.activation(out=o[:, c, 0:w], in_=s[:, c, 0:w], func=AF.Copy,
                                                   scale=vec[:, goff + c:goff + c + 1]), r=[sb_, G['vecb']], w=[obb])
            k.dma(STQ, Dv[:, :, t0:t0 + w], o[:, :, 0:w], r=[obb])


def st_rope(k, cfg, G, A, RAW, SW, NCH, DST, nrows=128):
    with k.stage("rope") as st:
        ct = [st.sb([128, 512], F32) for _ in range(2)]
        stt = [st.sb([128, 512], F32) for _ in range(2)]
        a = [st.sb([128, 512], BF16) for _ in range(2)]
        b = [st.sb([128, 512], BF16) for _ in range(2)]
        p = [st.sb([128, 512], F32) for _ in range(2)]
        o = [st.sb([128, 512], BF16) for _ in range(2)]
        n = 0
        R = nrows
        for i, (t0, w, kind) in enumerate(cfg.tiles):
            (c_, cb_), (s_, sb_) = ct[i % 2], stt[i % 2]
            k.dma('sp', c_[:, 0:w], A['ropeC'][:, t0:t0 + w], w=[cb_])
            k.dma('sp', s_[:, 0:w], A['ropeS'][:, t0:t0 + w], w=[sb_])
            for c in range(NCH):
                (x, xb), (y, yb), (q, qb), (oo, ob) = a[n % 2], b[n % 2], p[n % 2], o[n % 2]
                k.dma('sp', x[0:R, 0:w], RAW[c * 128:c * 128 + R, t0:t0 + w], w=[xb])
                k.dma('sp', y[0:R, 0:w], SW[c * 128:c * 128 + R, t0:t0 + w], w=[yb])
                e1, e2 = ('dve', 'pool') if n % 2 == 0 else ('pool', 'dve')
                k.op(e1, lambda E: E.tensor_tensor(out=q[0:R, 0:w], in0=x[0:R, 0:w], in1=c_[0:R, 0:w], op=ALU.mult),
                     r=[xb, cb_], w=[qb])
                k.op(e2, lambda E: E.tensor_tensor(out=x[0:R, 0:w], in0=y[0:R, 0:w], in1=s_[0:R, 0:w], op=ALU.mult),
                     r=[yb, sb_, xb], w=[xb])
                k.op(e1, lambda E: E.tensor_tensor(out=oo[0:R, 0:w], in0=q[0:R, 0:w], in1=x[0:R, 0:w], op=ALU.add),
                     r=[qb, xb], w=[ob])
                k.dma(STQ, DST[c * R:(c + 1) * R, t0:t0 + w], oo[0:R, 0:w], r=[ob])
                n += 1


def st_vproj(k, cfg, G, A, l, CKV, VTOK):
    KVC, H = cfg.KVC, cfg.Hl
    NV = H * 128
    Cv = fm(CKV)
    with k.stage("vproj") as st:
        w32, w32b = st.sb([128, KVC, NV], F32)
        wbf, wbfb = st.sb([128, KVC, NV], BF16)
        k.dma('sp', w32[:], A['w_uv'][l], w=[w32b])
        k.op('pool', lambda E: E.tensor_copy(out=wbf[:], in_=w32[:]), r=[w32b], w=[wbfb])
        ac = [st.sb([128, KVC, 512], BF16) for _ in range(2)]
        vs = [st.sb([128, NV], BF16) for _ in range(2)]
        banks = [st.ps() for _ in range(4)]
        nb = 0
        n = 0
        for i, (t0, w, kind) in enumerate(cfg.tiles):
            a, ab = ac[i % 2]
            k.dma('sp', a[:, :, 0:w], Cv[:, :, t0:t0 + w], w=[ab])
            for tb in range(w // 128):
                v, vb = vs[n % 2]
                for n0 in range(0, NV, 512):
                    nw = min(512, NV - n0)
                    ps, psb = banks[nb % 4]
                    for kc in range(KVC):
                        k.op('pe', lambda E: E.matmul(ps[:, 0:nw], a[:, kc, tb * 128:(tb + 1) * 128], wbf[:, kc, n0:n0 + nw],
                                                      start=(kc == 0), stop=(kc == KVC - 1)),
                             r=[ab, wbfb], w=[psb], acc=True)
                    if nb % 2 == 0:
                        k.op('act', lambda E: E.activation(out=v[:, n0:n0 + nw], in_=ps[:, 0:nw], func=AF.Copy),
                             r=[psb], w=[vb])
                    else:
                        k.op('dve', lambda E: E.tensor_copy(out=v[:, n0:n0 + nw], in_=ps[:, 0:nw]), r=[psb], w=[vb])
                    nb += 1
                k.dma(STQ, VTOK[t0 + tb * 128:t0 + (tb + 1) * 128, :], v[:], r=[vb])
                n += 1


def st_attn(k, cfg, G, QN, QP, KN, KPE, VTOK, O):
    H, T, CTX = cfg.Hl, cfg.T, cfg.CTX
    NKT = T // 128
    with k.stage("attn") as st:
        kpe, kpeb = st.sb([64, T], BF16)
        k.dma('sp', kpe[:], KPE[0:64, :], w=[kpeb])
        knt = [st.sb([128, T], BF16) for _ in range(2)]
        vt = [st.sb([128, NKT, 128], BF16) for _ in range(2)]
        qnt = [st.sb([128, T], BF16) for _ in range(2)]
        qpt = [st.sb([64, T], BF16) for _ in range(2)]
        pts = [st.sb([128, 512], BF16) for _ in range(3)]
        rl = [st.sb([128, 512], F32) for _ in range(2)]
        ot = [st.sb([128, 512], BF16) for _ in range(2)]
        psS = [st.ps() for _ in range(3)]
        psO = [st.ps() for _ in range(2)]
        psL = [st.ps() for _ in range(2)]
        ones_bf, cb = G['ones_bf'], G['cb']
        Vv = VTOK.rearrange("(kt p) f -> p kt f", p=128)
        ns = 0
        nq = 0
        for h in range(H):
            (kn, knb), (v, vb), (qn, qnb), (qp, qpb) = knt[h % 2], vt[h % 2], qnt[h % 2], qpt[h % 2]
            k.dma('sp', kn[:], KN[h * 128:(h + 1) * 128, :], w=[knb])
            k.dma('sp', v[:], Vv[:, :, h * 128:(h + 1) * 128], w=[vb])
            k.dma('sp', qn[:], QN[h * 128:(h + 1) * 128, :], w=[qnb])
            k.dma('sp', qp[:], QP[h * 64:(h + 1) * 64, :], w=[qpb])
            for (t0, w, kind) in cfg.tiles:
                kts = list(range(CTX // 128)) if kind else list(range(NKT))
                pO, pOb = psO[nq % 2]
                pL, pLb = psL[nq % 2]

                def emit_s(kt, slot):
                    pS, pSb = psS[slot % 3]
                    k.op('pe', lambda E: E.matmul(pS[:, 0:w], kn[:, kt * 128:(kt + 1) * 128], qn[:, t0:t0 + w],
                                                  start=True, stop=False), r=[knb, qnb], w=[pSb], acc=True)
                    k.op('pe', lambda E: E.matmul(pS[:, 0:w], kpe[:, kt * 128:(kt + 1) * 128], qp[:, t0:t0 + w],
                                                  start=False, stop=True), r=[kpeb, qpb], w=[pSb], acc=True)

                emit_s(kts[0], ns)
                for i, kt in enumerate(kts):
                    slot = ns + i
                    if i + 1 < len(kts):
                        emit_s(kts[i + 1], slot + 1)
                    pS, pSb = psS[slot % 3]
                    pt, ptb = pts[slot % 3]
                    k.op('act', lambda E: E.activation(out=pt[:, 0:w], in_=pS[:, 0:w], func=AF.Exp, scale=float(cfg.SCALE)),
                         r=[pSb], w=[ptb])
                    k.op('pe', lambda E: E.matmul(pO[:, 0:w], v[:, kt, :], pt[:, 0:w], start=(i == 0), stop=(i == len(kts) - 1)),
                         r=[vb, ptb], w=[pOb], acc=True)
                    k.op('pe', lambda E: E.matmul(pL[:, 0:w], ones_bf, pt[:, 0:w], start=(i == 0), stop=(i == len(kts) - 1)),
                         r=[cb, ptb], w=[pLb], acc=True)
                ns += len(kts)
                r_, rb = rl[nq % 2]
                o, ob = ot[nq % 2]
                k.op('dve', lambda E: E.reciprocal(out=r_[:, 0:w], in_=pL[:, 0:w]), r=[pLb], w=[rb])
                k.op('dve', lambda E: E.tensor_tensor(out=o[:, 0:w], in0=pO[:, 0:w], in1=r_[:, 0:w], op=ALU.mult),
                     r=[pOb, rb], w=[ob])
                k.dma(STQ, O[h * 128:(h + 1) * 128, t0:t0 + w], o[:, 0:w], r=[ob])
                nq += 1


def st_ssmconv(k, cfg, G, XBC, DTR, XH, BC, XTOK, BTOK, DTTOK):
    SI, SG, CCC, SIC = cfg.SIl, cfg.SGl, cfg.CCCl, cfg.SICl
    NB = SG
    vec = G['vec']
    with k.stage("ssmconv") as st:
        p1 = [st.sb([128, 514], BF16) for _ in range(2)]
        cv = [st.sb([128, 512], F32) for _ in range(2)]
        ob = [st.sb([128, 512], BF16) for _ in range(3)]
        xs = [st.sb([128, 4, SI], BF16) for _ in range(1)]
        bs = [st.sb([128, 4, NB * 128], BF16) for _ in range(1)]
        pT = [st.ps([128, 512], BF16) for _ in range(2)]
        pD = [st.ps() for _ in range(1)]
        dtt = [st.sb([128, 512], F32) for _ in range(2)]
        dts = [st.sb([128, 4, 128], F32) for _ in range(2)]
        ident_bf, ident_f, cb = G['ident_bf'], G['ident_f'], G['cb']
        rows = G['rows']
        n = 0
        for i, (t0, w, kind) in enumerate(cfg.tiles):
            s0, s1 = cfg.seq(kind)
            nblk = w // 128
            xst, xsb = xs[0]
            bst, bsb = bs[0]
            for c in range(CCC):
                wts = vec[:, cfg.V['ssmconv'] + 3 * c: cfg.V['ssmconv'] + 3 * c + 3]
                bias = vec[:, cfg.V['ssmb'] + c: cfg.V['ssmb'] + c + 1]
                (a, ab), (cvt, cvb), (o, obb) = p1[n % 2], cv[n % 2], ob[n % 3]
                load_halo(k, 'sp', a, ab, XBC[c * 128:(c + 1) * 128, :], t0, w, s0, s1)
                conv3(k, 'dve', cvt, cvb, a, ab, w, wts, [G['vecb']])
                k.op('act', lambda E: E.activation(out=o[:, 0:w], in_=cvt[:, 0:w], func=AF.Silu, bias=bias, scale=1.0),
                     r=[cvb, G['vecb']], w=[obb])
                if c < SIC:
                    k.dma(STQ, XH[c * 128:(c + 1) * 128, t0:t0 + w], o[:, 0:w], r=[obb])
                else:
                    k.dma(STQ, BC[(c - SIC) * 128:(c - SIC + 1) * 128, t0:t0 + w], o[:, 0:w], r=[obb])
                if c < SIC + NB:
                    pt, ptb = pT[n % 2]
                    for tb in range(nblk):
                        k.op('pe', lambda E: E.transpose(pt[:, tb * 128:(tb + 1) * 128], o[:, tb * 128:(tb + 1) * 128], ident_bf),
                             r=[obb, cb], w=[ptb])
                    if c < SIC:
                        dstt, dstb, cc = xst, xsb, c
                    else:
                        dstt, dstb, cc = bst, bsb, c - SIC
                    k.op('dve' if n % 2 else 'act',
                         (lambda E: E.tensor_copy(out=dstt[:, 0:nblk, cc * 128:(cc + 1) * 128],
                                                  in_=pt[:, 0:w].rearrange("p (b f) -> p b f", f=128))) if n % 2 else
                         (lambda E: E.activation(out=dstt[:, 0:nblk, cc * 128:(cc + 1) * 128],
                                                 in_=pt[:, 0:w].rearrange("p (b f) -> p b f", f=128), func=AF.Copy)),
                         r=[ptb], w=[dstb])
                n += 1
            k.dma(STQ, XTOK[t0:t0 + w, :].rearrange("(b p) f -> p b f", p=128), xst[:, 0:nblk, :], r=[xsb])
            k.dma(STQ, BTOK[t0:t0 + w, :].rearrange("(b p) f -> p b f", p=128), bst[:, 0:nblk, :], r=[bsb])
            d, db = dtt[i % 2]
            ds_, dsb = dts[i % 2]
            pd, pdb = pD[0]
            k.dma('sp', d[:, 0:w], DTR[:, t0:t0 + w], w=[db])
            for tb in range(nblk):
                k.op('pe', lambda E: E.transpose(pd[:, tb * 128:(tb + 1) * 128], d[:, tb * 128:(tb + 1) * 128], ident_f),
                     r=[db, cb], w=[pdb])
            k.op('dve', lambda E: E.tensor_tensor(out=ds_[:, 0:nblk, :], in0=pd[:, 0:w].rearrange("p (b f) -> p b f", f=128),
                                                  in1=rows[:, 0:1, :].to_broadcast([128, nblk, 128]), op=ALU.add),
                 r=[pdb, G['rowsb']], w=[dsb])
            k.op('act', lambda E: E.activation(out=ds_[:, 0:nblk, :], in_=ds_[:, 0:nblk, :], func=AF.Exp), r=[dsb], w=[dsb])
            k.op('act', lambda E: E.activation(out=ds_[:, 0:nblk, :], in_=ds_[:, 0:nblk, :], func=AF.Ln, bias=G['one'][:, 0:1],
                                               scale=1.0), r=[dsb], w=[dsb])
            k.dma(STQ, DTTOK[t0:t0 + w, :].rearrange("(b p) f -> p b f", p=128), ds_[:, 0:nblk, :], r=[dsb])


def st_ssd(k, cfg, G, d, XTOK, BTOK, DTTOK, BC, YOUT):
    SH, SG, SI, T, CTX = cfg.SHl, cfg.SGl, cfg.SIl, cfg.T, cfg.CTX
    NCH = T // 128
    NCC = CTX // 128
    order = list(range(NCH)) if d == 0 else (list(range(NCC - 1, -1, -1)) + list(range(NCH - 1, NCC - 1, -1)))
    U = G['Uf'] if d == 0 else G['Ub']
    M01 = G['Mf'] if d == 0 else G['Mb']
    ones_f, cb, rows = G['ones_f'], G['cb'], G['rows']
    BCv = BC.rearrange("(x g n) t -> n x g t", x=2, n=128)
    with k.stage("ssd%d" % d) as st:
        state, stb = st.sb([128, SH, 64], F32)
        prev, prevb = st.sb([128, SH, 64], BF16)
        k.op('dve', lambda E: E.memset(state[:], 0.0), w=[stb])
        k.op('pool', lambda E: E.memset(prev[:], 0.0), w=[prevb])
        xts = [st.sb([128, SH, 64], BF16) for _ in range(2)]
        bts = [st.sb([128, SG * 128], BF16) for _ in range(2)]
        dtts = [st.sb([128, 128], F32) for _ in range(2)]
        bcs = [st.sb([128, 2, SG, 128], BF16) for _ in range(2)]
        adt, adtb = st.sb([128, SH], F32)
        nac, nacb = st.sb([128, SH], F32)
        tmp, tmpb = st.sb([128, SH], F32)
        decs, decsb = st.sb([128, SH], F32)
        cd, cdb = st.sb([128, SH], F32)
        xdt, xdtb = st.sb([128, SH, 64], BF16)
        xdd, xddb = st.sb([128, SH, 64], BF16)
        cbm, cbmb = st.sb([128, SG, 128], BF16)
        abt = [st.sb([128, 4, 128], F32) for _ in range(2)]
        lmt = [st.sb([128, 4, 128], BF16) for _ in range(2)]
        mtt = [st.sb([128, 4, 128], BF16) for _ in range(2)]
        e4t = [st.sb([64, 4, 128], F32) for _ in range(2)]
        t4t = [st.sb([64, 4, 128], F32) for _ in range(2)]
        yst = [st.sb([64, SH, 128], F32) for _ in range(2)]
        psA, psAb = st.ps()
        psCB, psCBb = st.ps()
        psR = [st.ps() for _ in range(2)]
        psYD = [st.ps() for _ in range(1)]
        psYO = [st.ps() for _ in range(1)]
        psS = [st.ps() for _ in range(2)]
        nR = 0
        nS = 0
        for ci, c in enumerate(order):
            tc0 = c * 128
            (xt, xtb), (bt, btb), (dtt, dttb), (bc, bcb) = xts[ci % 2], bts[ci % 2], dtts[ci % 2], bcs[ci % 2]
            k.dma('sp', xt[:].rearrange("p h q -> p (h q)"), XTOK[tc0:tc0 + 128, :], w=[xtb])
            k.dma('sp', bt[:], BTOK[tc0:tc0 + 128, :], w=[btb])
            k.dma('sp', dtt[:], DTTOK[tc0:tc0 + 128, :], w=[dttb])
            k.dma('sp', bc[:], BCv[:, :, :, tc0:tc0 + 128], w=[bcb])
            dtd = dtt[:, d * SH:(d + 1) * SH]
            k.op('dve', lambda E: E.tensor_tensor(out=adt[:], in0=dtd, in1=rows[:, 1, d * SH:(d + 1) * SH], op=ALU.mult),
                 r=[dttb, G['rowsb']], w=[adtb])
            k.op('pe', lambda E: E.matmul(psA[:, 0:SH], U, adt[:], start=True, stop=True), r=[adtb, cb], w=[psAb])
            k.op('pe', lambda E: E.matmul(psA[:, SH:2 * SH], ones_f, adt[:], start=True, stop=True), r=[adtb, cb], w=[psAb], acc=True)
            k.op('dve', lambda E: E.tensor_scalar(out=nac[:], in0=psA[:, 0:SH], scalar1=-1.0, scalar2=None, op0=ALU.mult),
                 r=[psAb], w=[nacb])
            k.op('dve', lambda E: E.tensor_tensor(out=tmp[:], in0=psA[:, SH:2 * SH], in1=nac[:], op=ALU.add),
                 r=[psAb, nacb], w=[tmpb])
            k.op('act', lambda E: E.activation(out=decs[:], in_=tmp[:], func=AF.Exp), r=[tmpb], w=[decsb])
            k.op('act', lambda E: E.activation(out=cd[:], in_=psA[:, SH:2 * SH], func=AF.Exp), r=[psAb], w=[cdb])
            k.op('pool', lambda E: E.tensor_tensor(out=xdt[:], in0=xt[:], in1=dtd.unsqueeze(2).to_broadcast([128, SH, 64]),
                                                   op=ALU.mult), r=[xtb, dttb], w=[xdtb])
            k.op('pool', lambda E: E.tensor_tensor(out=xdd[:], in0=xdt[:], in1=decs[:].unsqueeze(2).to_broadcast([128, SH, 64]),
                                                   op=ALU.mult), r=[xdtb, decsb], w=[xddb])
            for g0 in range(0, SG, 4):
                ng = min(4, SG - g0)
                for g in range(g0, g0 + ng):
                    k.op('pe', lambda E: E.matmul(psCB[:, (g - g0) * 128:(g - g0 + 1) * 128], bc[:, 0, g, :], bc[:, 1, g, :],
                                                  start=True, stop=True), r=[bcb], w=[psCBb], acc=True)
                k.op('dve', lambda E: E.tensor_tensor(out=cbm[:, g0:g0 + ng, :],
                                                      in0=psCB[:, 0:ng * 128].rearrange("p (g l) -> p g l", l=128),
                                                      in1=M01.unsqueeze(1).to_broadcast([128, ng, 128]), op=ALU.mult),
                     r=[psCBb, cb], w=[cbmb])
            ys, ysb = yst[ci % 2]
            for h0 in range(0, SH, 4):
                g = h0 // 8
                pR, pRb = psR[nR % 2]
                (ab, abb), (lm, lmb), (mt, mtb), (e4, e4b), (t4, t4b) = abt[nR % 2], lmt[nR % 2], mtt[nR % 2], e4t[nR % 2], t4t[nR % 2]
                nR += 1
                for j in range(4):
                    h = h0 + j
                    k.op('pe', lambda E: E.matmul(pR[:, j * 128:(j + 1) * 128], adt[:, h:h + 1].to_broadcast([128, 128]), U,
                                                  start=True, stop=True), r=[adtb, cb], w=[pRb], acc=True)
                for j in range(4):
                    h = h0 + j
                    k.op('dve', lambda E: E.tensor_scalar(out=ab[:, j, :], in0=pR[:, j * 128:(j + 1) * 128],
                                                          scalar1=nac[:, h:h + 1], scalar2=0.0, op0=ALU.add, op1=ALU.min),
                         r=[pRb, nacb], w=[abb])
                k.op('act', lambda E: E.activation(out=lm[:], in_=ab[:], func=AF.Exp), r=[abb], w=[lmb])
                k.op('act', lambda E: E.activation(out=e4[:], in_=pR[0:64, :].rearrange("p (j l) -> p j l", l=128), func=AF.Exp),
                     r=[pRb], w=[e4b])
                k.op('pool', lambda E: E.tensor_tensor(out=mt[:], in0=lm[:], in1=cbm[:, g:g + 1, :].to_broadcast([128, 4, 128]),
                                                       op=ALU.mult), r=[lmb, cbmb], w=[mtb])
                pYD, pYDb = psYD[0]
                pYO, pYOb = psYO[0]
                for j in range(4):
                    h = h0 + j
                    k.op('pe', lambda E: E.matmul(pYD[0:64, j * 128:(j + 1) * 128], xdt[:, h, :], mt[:, j, :], start=True, stop=True),
                         r=[xdtb, mtb], w=[pYDb], acc=True)
                    k.op('pe', lambda E: E.matmul(pYO[0:64, j * 128:(j + 1) * 128], prev[:, h, :], bc[:, 1, g, :], start=True, stop=True),
                         r=[prevb, bcb], w=[pYOb], acc=True)
                k.op('dve', lambda E: E.tensor_tensor(out=t4[:], in0=pYO[0:64, :].rearrange("p (j l) -> p j l", l=128), in1=e4[:],
                                                      op=ALU.mult), r=[pYOb, e4b], w=[t4b])
                k.op('dve', lambda E: E.tensor_tensor(out=ys[:, h0:h0 + 4, :], in0=t4[:],
                                                      in1=pYD[0:64, :].rearrange("p (j l) -> p j l", l=128), op=ALU.add),
                     r=[t4b, pYDb], w=[ysb])
            k.dma(STQ, YOUT[:, tc0:tc0 + 128].rearrange("(h p) l -> p h l", p=64), ys[:], r=[ysb])
            for g in range(SG):
                pS, pSb = psS[nS % 2]
                nS += 1
                k.op('pe', lambda E: E.matmul(pS[:], bt[:, g * 128:(g + 1) * 128],
                                              xdd[:, g * 8:(g + 1) * 8, :].rearrange("p h q -> p (h q)"), start=True, stop=True),
                     r=[btb, xddb], w=[pSb])
                k.op('dve', lambda E: E.tensor_tensor(out=state[:, g * 8:(g + 1) * 8, :], in0=state[:, g * 8:(g + 1) * 8, :],
                                                      in1=cd[:, g * 8:(g + 1) * 8].unsqueeze(2).to_broadcast([128, 8, 64]),
                                                      op=ALU.mult), r=[stb, cdb], w=[stb])
                k.op('dve', lambda E: E.tensor_tensor(out=state[:, g * 8:(g + 1) * 8, :], in0=state[:, g * 8:(g + 1) * 8, :],
                                                      in1=pS[:].rearrange("p (h q) -> p h q", q=64), op=ALU.add),
                     r=[stb, pSb], w=[stb])
            k.op('act', lambda E: E.activation(out=prev[:], in_=state[:], func=AF.Copy), r=[stb], w=[prevb])


def st_ssmgate(k, cfg, G, YF, YB, XH, SZ, YG):
    SIC = cfg.SICl
    vec = G['vec']
    W = 256
    YFv, YBv, XHv, SZv, YGv = fm(YF), fm(YB), fm(XH), fm(SZ), fm(YG)
    with k.stage("ssmgate") as st:
        yfs = [st.sb([128, SIC, W], F32) for _ in range(2)]
        ybs = [st.sb([128, SIC, W], F32) for _ in range(2)]
        xhs = [st.sb([128, SIC, W], BF16) for _ in range(2)]
        szs = [st.sb([128, SIC, W], BF16) for _ in range(2)]
        os_ = [st.sb([128, SIC, W], BF16) for _ in range(2)]
        for i, t0 in enumerate(range(0, cfg.T, W)):
            (yf, yfb), (yb, ybb), (xh, xhb), (sz, szb), (o, ob) = yfs[i % 2], ybs[i % 2], xhs[i % 2], szs[i % 2], os_[i % 2]
            k.dma('sp', yf[:], YFv[:, :, t0:t0 + W], w=[yfb])
            k.dma('sp', yb[:], YBv[:, :, t0:t0 + W], w=[ybb])
            k.dma('sp', xh[:], XHv[:, :, t0:t0 + W], w=[xhb])
            k.dma('sp', sz[:], SZv[:, :, t0:t0 + W], w=[szb])
            k.op('pool', lambda E: E.tensor_tensor(out=yf[:], in0=yf[:], in1=yb[:], op=ALU.add), r=[yfb, ybb], w=[yfb])
            for c in range(SIC):
                k.op('dve',
                     lambda E: E.scalar_tensor_tensor(out=yf[:, c, :], in0=xh[:, c, :], scalar=vec[:, cfg.V['dsk'] + c:cfg.V['dsk'] + c + 1],
                                                      in1=yf[:, c, :], op0=ALU.mult, op1=ALU.add),
                     r=[xhb, yfb, G['vecb']], w=[yfb])
            k.op('pool', lambda E: E.tensor_tensor(out=o[:], in0=yf[:], in1=sz[:], op=ALU.mult), r=[yfb, szb], w=[ob])
            k.dma(STQ, YGv[:, :, t0:t0 + W], o[:], r=[ob])


def build_program(cfg, dbg=()):
    nc = bass.Bass("TRN2", target_bir_lowering=False)
    k = K(nc)
    D, T, S, DC, L, H = cfg.D, cfg.T, cfg.S, cfg.DC, cfg.DEPTH, cfg.H
    Dl, DCl, Hl, SIl, TP = cfg.Dl, cfg.DCl, cfg.Hl, cfg.SIl, cfg.TP
    A = {}

    def inp(name, shape):
        A[name] = nc.dram_tensor(name, list(shape), F32, kind="ExternalInput").ap()

    inp('xin', [D, T]); inp('ccol', [128, DC, 2]); inp('ropeC', [128, T]); inp('ropeS', [128, T])
    inp('cst', [128, 6, 128])
    inp('w_ada', [L, 6 * DC, 128, DC, 128]); inp('b_ada', [L, 128, 6 * DC])
    inp('w_in', [L, cfg.NJIN, 128, DC, 128]); inp('w_sc_out', [L, DCl, 128, DC, 128])
    inp('w_uq', [L, 2 * Hl, 128, cfg.QLC, 128]); inp('w_ukn', [L, Hl, 128, cfg.KVC, 128])
    inp('w_uv', [L, 128, cfg.KVC, Hl * 128]); inp('w_mla_out', [L, DCl, 128, H, 128])
    inp('w_ssm_out', [L, DCl, 128, cfg.SIC, 128]); inp('w_out', [L, DCl, 128, DC, 128])
    inp('w_up', [L, 2 * cfg.FFCl, 128, DC, 128]); inp('w_down', [L, DCl, 128, cfg.FFC, 128])
    inp('vec', [L, 128, cfg.NV]); inp('rowv', [L, 2, 128])
    yout = nc.dram_tensor("yout", [D, S], F32, kind="ExternalOutput").ap()

    def scr(name, shape, dt):
        kind = "ExternalOutput" if name in dbg else "Internal"
        return nc.dram_tensor(name, list(shape), dt, kind=kind).ap()

    Sx = dict(
        u=scr('u', [D, T], BF16), abc=scr('abc', [3 * Dl, T], BF16), ql=scr('ql', [cfg.QL, T], BF16),
        kvl=scr('kvl', [cfg.KVL, T], BF16), krr=scr('krr', [128, T], BF16), krs=scr('krs', [128, T], BF16),
        sz=scr('sz', [SIl, T], BF16), xbc=scr('xbc', [cfg.CCl, T], BF16), dtr=scr('dtr', [128, T], F32),
        gt=scr('gt', [3 * Dl, T], BF16), ain=scr('ain', [Dl, T], BF16), ya=scr('ya', [Dl, T], BF16),
        cq=scr('cq', [cfg.QL, T], BF16), qn=scr('qn', [Hl * 128, T], BF16), qpr=scr('qpr', [Hl * 64, T], BF16),
        qps=scr('qps', [Hl * 64, T], BF16), qp=scr('qp', [Hl * 64, T], BF16), ckv=scr('ckv', [cfg.KVL, T], BF16),
        kn=scr('kn', [Hl * 128, T], BF16), vtok=scr('vtok', [T, Hl * 128], BF16), kpe=scr('kpe', [64, T], BF16),
        o=scr('o', [Hl * 128, T], BF16), yb=scr('yb', [Dl, T], BF16), xh=scr('xh', [SIl, T], BF16),
        bc=scr('bc', [2 * cfg.SGl * 128, T], BF16), xtok=scr('xtok', [T, SIl], BF16),
        btok=scr('btok', [T, cfg.SGl * 128], BF16), dttok=scr('dttok', [T, 128], F32),
        yf=scr('yf', [SIl, T], F32), ybw=scr('ybw', [SIl, T], F32), yg=scr('yg', [SIl, T], BF16),
        yc=scr('yc', [Dl, T], BF16), mg=scr('mg', [Dl, T], BF16), yx=scr('yx', [Dl, T], BF16),
        xmid=scr('xmid', [D, T], F32), xres=scr('xres', [D, T], F32), gu=scr('gu', [2 * cfg.FFl, T], BF16),
        hh=scr('hh', [cfg.FFl, T], BF16), ys=scr('ys', [cfg.SI, T], BF16),
    )
    if TP > 1:
        Sx.update(ainF=scr('ainF', [D, T], BF16), oF=scr('oF', [H * 128, T], BF16), ygF=scr('ygF', [cfg.SI, T], BF16),
                  mgF=scr('mgF', [D, T], BF16), yxF=scr('yxF', [D, T], BF16), hhF=scr('hhF', [cfg.FF, T], BF16))

    def gather(nm):
        if TP == 1:
            return Sx[nm]
        k.allgather(cfg, Sx[nm], Sx[nm + 'F'])
        return Sx[nm + 'F']

    G = {}
    cst = nc.alloc_sbuf_tensor("sb_cst", [128, 6, 128], F32)
    cstbf = nc.alloc_sbuf_tensor("sb_cstbf", [128, 2, 128], BF16)
    G['cb'] = Buf()
    G['ones_f'], G['ident_f'], G['Uf'], G['Ub'], G['Mf'], G['Mb'] = (cst[:, i, :] for i in range(6))
    G['ones_bf'], G['ident_bf'] = cstbf[:, 0, :], cstbf[:, 1, :]
    G['mods'] = nc.alloc_sbuf_tensor("sb_mods", [128, 6 * DC, 2], F32)
    G['modsb'] = Buf()
    G['vec'] = nc.alloc_sbuf_tensor("sb_vec", [128, cfg.NV], F32)
    G['vecb'] = Buf()
    G['rows'] = nc.alloc_sbuf_tensor("sb_rows", [128, 2, 128], F32)
    G['rowsb'] = Buf()
    small = nc.alloc_sbuf_tensor("sb_smallc", [128, 2], F32)
    G['eps'] = small[:, 0:1]
    G['one'] = small[:, 1:2]
    k.dma('sp', cst[:], A['cst'], w=[G['cb']])
    k.op('dve', lambda E: E.tensor_copy(out=cstbf[:], in_=cst[:, 0:2, :]), r=[G['cb']], w=[G['cb']])
    k.op('dve', lambda E: E.memset(small[:, 0:1], float(cfg.EPS)), w=[G['cb']])
    k.op('dve', lambda E: E.memset(small[:, 1:2], 1.0), w=[G['cb']])
    k.barrier()

    V = cfg.V
    for l in range(L):
        last = (l == L - 1)
        X = A['xin'] if l == 0 else Sx['xres']
        st_ada(k, cfg, G, A, l)
        st_lnmod(k, cfg, G, X, Sx['u'], 0, 1)

        dsts = []
        for c in range(3 * DCl):
            dsts.append((Sx['abc'][c * 128:(c + 1) * 128, :], 'copy'))
        for c in range(cfg.QLC):
            dsts.append((Sx['ql'][c * 128:(c + 1) * 128, :], 'copy'))
        for c in range(cfg.KVC):
            dsts.append((Sx['kvl'][c * 128:(c + 1) * 128, :], 'copy'))
        dsts.append((Sx['krr'], 'copy'))
        dsts.append((Sx['krs'], 'copy'))
        for c in range(cfg.SICl):
            dsts.append((Sx['sz'][c * 128:(c + 1) * 128, :], 'silu'))
        for c in range(cfg.CCCl):
            dsts.append((Sx['xbc'][c * 128:(c + 1) * 128, :], 'copy'))
        dsts.append((Sx['dtr'], 'copy'))
        for c in range(3 * DCl):
            dsts.append((Sx['gt'][c * 128:(c + 1) * 128, :], 'sigmoid'))
        assert len(dsts) == cfg.NJIN
        linear(k, cfg, "inproj", Sx['u'], DC, A['w_in'][l], cfg.NJIN, lambda j: dsts[j])

        st_mixa(k, cfg, G, Sx['abc'], Sx['ain'])
        ainF = gather('ain')
        linear(k, cfg, "scout", ainF, DC, A['w_sc_out'][l], DCl, lambda j: (Sx['ya'][j * 128:(j + 1) * 128, :], 'copy'))

        st_rms(k, cfg, G, Sx['ql'], cfg.QLC, Sx['cq'], V['gqa'])
        st_rms(k, cfg, G, Sx['kvl'], cfg.KVC, Sx['ckv'], V['gkva'])

        def qdst(j):
            if j < Hl:
                return (Sx['qn'][j * 128:(j + 1) * 128, :], 'copy')
            if j < Hl + Hl // 2:
                c = j - Hl
                return (Sx['qpr'][c * 128:(c + 1) * 128, :], 'copy')
            c = j - Hl - Hl // 2
            return (Sx['qps'][c * 128:(c + 1) * 128, :], 'copy')
        linear(k, cfg, "uq", Sx['cq'], cfg.QLC, A['w_uq'][l], 2 * Hl, qdst)
        linear(k, cfg, "ukn", Sx['ckv'], cfg.KVC, A['w_ukn'][l], Hl, lambda j: (Sx['kn'][j * 128:(j + 1) * 128, :], 'copy'))
        st_vproj(k, cfg, G, A, l, Sx['ckv'], Sx['vtok'])
        st_rope(k, cfg, G, A, Sx['qpr'], Sx['qps'], Hl // 2, Sx['qp'], 128)
        st_rope(k, cfg, G, A, Sx['krr'], Sx['krs'], 1, Sx['kpe'], 64)
        st_attn(k, cfg, G, Sx['qn'], Sx['qp'], Sx['kn'], Sx['kpe'], Sx['vtok'], Sx['o'])
        oF = gather('o')
        linear(k, cfg, "mlaout", oF, H, A['w_mla_out'][l], DCl, lambda j: (Sx['yb'][j * 128:(j + 1) * 128, :], 'copy'))

        st_ssmconv(k, cfg, G, Sx['xbc'], Sx['dtr'], Sx['xh'], Sx['bc'], Sx['xtok'], Sx['btok'], Sx['dttok'])
        st_ssd(k, cfg, G, 0, Sx['xtok'], Sx['btok'], Sx['dttok'], Sx['bc'], Sx['yf'])
        st_ssd(k, cfg, G, 1, Sx['xtok'], Sx['btok'], Sx['dttok'], Sx['bc'], Sx['ybw'])
        st_ssmgate(k, cfg, G, Sx['yf'], Sx['ybw'], Sx['xh'], Sx['sz'], Sx['yg'])
        ygF = gather('yg')
        st_rms(k, cfg, G, ygF, cfg.SIC, Sx['ys'], V['gssm'], W=256)
        linear(k, cfg, "ssmo", Sx['ys'], cfg.SIC, A['w_ssm_out'][l], DCl, lambda j: (Sx['yc'][j * 128:(j + 1) * 128, :], 'copy'))

        st_merge(k, cfg, G, Sx['gt'], Sx['ya'], Sx['yb'], Sx['yc'], Sx['mg'])
        mgF = gather('mg')
        linear(k, cfg, "wout", mgF, DC, A['w_out'][l], DCl, lambda j: (Sx['yx'][j * 128:(j + 1) * 128, :], 'copy'))
        yxF = gather('yx')
        st_postnorm(k, cfg, G, X, yxF, Sx['xmid'], 2, V['ln1g'], V['ln1b'], l)

        st_lnmod(k, cfg, G, Sx['xmid'], Sx['u'], 3, 4)
        linear(k, cfg, "wup", Sx['u'], DC, A['w_up'][l], 2 * cfg.FFCl, lambda j: (Sx['gu'][j * 128:(j + 1) * 128, :], 'copy'))
        st_ffconv(k, cfg, G, Sx['gu'], Sx['hh'])
        hhF = gather('hh')
        linear(k, cfg, "wdown", hhF, cfg.FFC, A['w_down'][l], DCl, lambda j: (Sx['yx'][j * 128:(j + 1) * 128, :], 'copy'))
        yxF = gather('yx')
        if last:
            st_postnorm(k, cfg, G, Sx['xmid'], yxF, yout, 5, V['ln2g'], V['ln2b'], l, only_x=True)
        else:
            st_postnorm(k, cfg, G, Sx['xmid'], yxF, Sx['xres'], 5, V['ln2g'], V['ln2b'], l)
    k.barrier()
    return nc, k


def wlayout(W):
    L, Kd, N = W.shape
    return np.ascontiguousarray(W.reshape(L, Kd // 128, 128, N // 128, 128).transpose(0, 3, 2, 1, 4))


def gather_cols(W, idx):
    idx = np.asarray(idx)
    out = W[:, :, np.maximum(idx, 0)]
    if (idx < 0).any():
        out = out.copy()
        out[:, :, idx < 0] = 0.0
    return out


def colvec(v, nch):
    L = v.shape[0]
    return v.reshape(L, nch, 128).transpose(0, 2, 1)


def prep_inputs(cfg, inp):
    f32 = np.float32
    D, DC, H, SH, L, TP = cfg.D, cfg.DC, cfg.H, cfg.SH, cfg.DEPTH, cfg.TP
    Hl, SHl = cfg.Hl, cfg.SHl
    g = {k_: np.asarray(v, dtype=f32) for k_, v in inp.items()}
    sw = np.arange(64).reshape(2, 2, 16)[:, ::-1, :].reshape(64)
    o = 0
    offs = {}
    for nm, n in (("sb", D), ("sc", D), ("sh", D), ("ql", cfg.QL), ("kvl", cfg.KVL), ("kr", 64), ("z", cfg.SI),
                  ("xs", cfg.SI), ("bs", cfg.SG * 128), ("cs", cfg.SG * 128), ("dt", 2 * SH), ("gt", 3 * D)):
        offs[nm] = o
        o += n
    pad = lambda n: [-1] * n

    def blk(off, n, r):
        return list(range(off + r * (n // TP), off + (r + 1) * (n // TP)))

    common = {}
    common['w_ada'] = wlayout(g['w_ada'])
    common['b_ada'] = np.ascontiguousarray(colvec(g['b_ada'], 6 * DC))
    ii = np.arange(128)
    cst = np.zeros((128, 6, 128), f32)
    cst[:, 0, :] = 1.0
    cst[:, 1, :] = np.eye(128)
    cst[:, 2, :] = (ii[:, None] <= ii[None, :])
    cst[:, 3, :] = (ii[:, None] >= ii[None, :])
    cst[:, 4, :] = (ii[:, None] <= ii[None, :])
    cst[:, 5, :] = (ii[:, None] >= ii[None, :])
    common['cst'] = cst
    S, GW = cfg.S, cfg.GRID_W
    pos = np.arange(S)
    row = (pos // GW).astype(np.float64)
    col = (pos % GW).astype(np.float64)
    inv = 10000.0 ** (-np.arange(16, dtype=np.float64) / 16)
    C64 = np.zeros((64, S)); S64 = np.zeros((64, S))
    for a_, p_ in enumerate((row, col)):
        ang = (p_[None, :].astype(np.float32) * inv[:, None].astype(np.float32)).astype(np.float32).astype(np.float64)
        for hf in range(2):
            C64[a_ * 32 + hf * 16:a_ * 32 + hf * 16 + 16] = np.cos(ang)
            S64[a_ * 32 + hf * 16:a_ * 32 + hf * 16 + 16] = np.sin(ang) * (-1.0 if hf == 0 else 1.0)
    rc = np.ones((128, cfg.T), f32); rs = np.zeros((128, cfg.T), f32)
    rc[0:64, cfg.CTX:] = C64; rc[64:128, cfg.CTX:] = C64
    rs[0:64, cfg.CTX:] = S64; rs[64:128, cfg.CTX:] = S64
    common['ropeC'] = rc
    common['ropeS'] = rs

    V = cfg.V
    per_r = []
    for r in range(TP):
        m = {}
        idx = blk(offs['sb'], D, r) + blk(offs['sc'], D, r) + blk(offs['sh'], D, r)
        idx += list(range(offs['ql'], offs['ql'] + cfg.QL)) + list(range(offs['kvl'], offs['kvl'] + cfg.KVL))
        idx += list(range(offs['kr'], offs['kr'] + 64)) + pad(64)
        idx += list(offs['kr'] + sw) + pad(64)
        idx += blk(offs['z'], cfg.SI, r)
        idx += blk(offs['xs'], cfg.SI, r) + blk(offs['bs'], cfg.SG * 128, r) + blk(offs['cs'], cfg.SG * 128, r)
        idx += blk(offs['dt'], SH, r) + blk(offs['dt'] + SH, SH, r) + pad(128 - 2 * SHl)
        for mth in range(3):
            idx += blk(offs['gt'] + mth * D, D, r)
        assert len(idx) == cfg.NJIN * 128
        m['w_in'] = wlayout(gather_cols(g['w_in'], idx))
        dcols = blk(0, D, r)
        m['w_sc_out'] = wlayout(gather_cols(g['w_sc_out'], dcols))
        heads = list(range(r * Hl, (r + 1) * Hl))
        qi = []
        for h in heads:
            qi += list(range(h * 192, h * 192 + 128))
        for h in heads:
            qi += list(range(h * 192 + 128, h * 192 + 192))
        for h in heads:
            qi += list(h * 192 + 128 + sw)
        m['w_uq'] = wlayout(gather_cols(g['w_uq'], qi))
        ki, vi = [], []
        for h in heads:
            ki += list(range(h * 256, h * 256 + 128))
            vi += list(range(h * 256 + 128, h * 256 + 256))
        m['w_ukn'] = wlayout(gather_cols(g['w_ukv'], ki))
        wv = gather_cols(g['w_ukv'], vi)
        m['w_uv'] = np.ascontiguousarray(wv.reshape(L, cfg.KVC, 128, Hl * 128).transpose(0, 2, 1, 3))
        m['w_mla_out'] = wlayout(gather_cols(g['w_mla_out'], dcols))
        m['w_ssm_out'] = wlayout(gather_cols(g['w_ssm_out'], dcols))
        m['w_out'] = wlayout(gather_cols(g['w_out'], dcols))
        m['w_up'] = wlayout(gather_cols(g['w_up'], blk(0, cfg.FF, r) + blk(cfg.FF, cfg.FF, r)))
        m['w_down'] = wlayout(gather_cols(g['w_down'], dcols))
        vec = np.zeros((L, 128, cfg.NV), f32)

        def put(nm, arr):
            vec[:, :, V[nm]:V[nm] + arr.shape[2]] = arr

        def conv_cols(wc, cols):
            n = len(cols) // 128
            return wc[:, :, cols].reshape(L, 3, n, 128).transpose(0, 3, 2, 1).reshape(L, 128, n * 3)
        put('scconv', conv_cols(g['w_sc_conv'], dcols))
        put('gqa', colvec(g['g_qa'], cfg.QLC))
        put('gkva', colvec(g['g_kva'], cfg.KVC))
        ccols = blk(0, cfg.SI, r) + blk(cfg.SI, cfg.SG * 128, r) + blk(cfg.SI + cfg.SG * 128, cfg.SG * 128, r)
        put('ssmconv', conv_cols(g['w_ssm_conv'], ccols))
        put('ssmb', colvec(g['b_ssm_conv'][:, ccols], cfg.CCCl))
        put('gssm', colvec(g['g_ssm_norm'], cfg.SIC))
        put('dsk', colvec(np.repeat(g['d_skip'], 64, axis=1)[:, blk(0, cfg.SI, r)], cfg.SICl))
        put('ln1g', colvec(g['ln1_g'], DC)); put('ln1b', colvec(g['ln1_b'], DC))
        put('ln2g', colvec(g['ln2_g'], DC)); put('ln2b', colvec(g['ln2_b'], DC))
        put('ffconv', conv_cols(g['w_ff_conv'], blk(0, cfg.FF, r)))
        m['vec'] = vec
        rowv = np.zeros((L, 2, 128), f32)
        hs = slice(r * SHl, (r + 1) * SHl)
        rowv[:, 0, 0:SHl] = g['dt_bias_f'][:, hs]; rowv[:, 0, SHl:2 * SHl] = g['dt_bias_b'][:, hs]
        rowv[:, 1, 0:SHl] = g['a_log_f'][:, hs]; rowv[:, 1, SHl:2 * SHl] = g['a_log_b'][:, hs]
        m['rowv'] = rowv
        per_r.append(m)
    maps = []
    for b in range(cfg.B):
        xin = np.ascontiguousarray(np.concatenate([g['ctx'][b].T, g['x'][b].T], axis=1))
        cc = np.stack([g['c'][b], g['c_ctx']], axis=0)
        ccol = np.ascontiguousarray(cc.reshape(2, DC, 128).transpose(2, 1, 0))
        for r in range(TP):
            m = dict(common)
            m.update(per_r[r])
            m['xin'] = xin
            m['ccol'] = ccol
            maps.append(m)
    return maps


def run_model(cfg, inputs, dbg=()):
    maps = prep_inputs(cfg, inputs)
    nc, k = build_program(cfg, dbg)
    ncore = cfg.B * cfg.TP
    res = run_bass_kernel_spmd(nc, maps, core_ids=list(range(ncore)))
    out = np.stack([np.ascontiguousarray(res.results[b * cfg.TP]["yout"].T) for b in range(cfg.B)], axis=0).astype(np.float32)
    return out, res


def kernel(**inputs):
    cfg = Cfg()
    out, _ = run_model(cfg, inputs)
    return out
```
